# Optimizing a Trainium2 kernel written in Bass

```python
import math
import jax, jax.numpy as jnp
from jax import lax
import numpy as np

D_MODEL = 2048
BATCH = 8
SEQ = 2048
DEPTH = 2
DEC_BATCH = 32
DEC_SEQ = 4
PAST_LEN = 8192
PAGE_SIZE = 128

N_HEADS = 8
HEAD_DIM = 64
V_DIM = 2 * HEAD_DIM
ATTN_DIM = N_HEADS * V_DIM
ATTN_SCALE = HEAD_DIM ** -0.5
LAMBDA_INIT = 0.8 - 0.6 * math.exp(-0.3 * 0)
Q_BLOCK = 128
CONV_A_DIM = D_MODEL // 2
CONV_A_WIDTH = 31
CONV_C_DIM = D_MODEL
CONV_C_WIDTH = 3
D_FF = 11 * D_MODEL // 4
W_IN_AB_COLS = 2 * CONV_A_DIM + 3 * ATTN_DIM
RMS_EPS = 1e-6
LN_EPS = 1e-5

kernel_name = "hybrid_conformer_diffattn_shortconv_decode_step"


def _rmsnorm(x, g):
    xf = x.astype(jnp.float32)
    y = xf * lax.rsqrt(jnp.mean(xf * xf, axis=-1, keepdims=True) + RMS_EPS)
    return (y * g.astype(jnp.float32)).astype(x.dtype)


def _layernorm(x, g, b):
    xf = x.astype(jnp.float32)
    mu = jnp.mean(xf, axis=-1, keepdims=True)
    var = jnp.mean(jnp.square(xf - mu), axis=-1, keepdims=True)
    y = (xf - mu) * lax.rsqrt(var + LN_EPS) * g.astype(jnp.float32) + b.astype(jnp.float32)
    return y.astype(x.dtype)


def _swiglu(x, w_gate, w_up, w_down):
    return (jax.nn.silu(x @ w_gate) * (x @ w_up)) @ w_down


def _dwconv_valid(x, w):
    c = x.shape[-1]
    return lax.conv_general_dilated(x, w[:, None, :].astype(x.dtype), window_strides=(1,), padding='VALID',
                                    dimension_numbers=('NWC', 'WIO', 'NWC'), feature_group_count=c)


def _alibi_slopes():
    return jnp.exp2(-8.0 * jnp.arange(1, N_HEADS + 1, dtype=jnp.float32) / N_HEADS)


def _diff_scores(q, k, q_pos, k_pos, slopes):
    s = jnp.einsum('bqhmd,bkhmd->mbhqk', q, k, preferred_element_type=jnp.float32) * ATTN_SCALE
    dist = (q_pos[:, None] - k_pos[None, :]).astype(jnp.float32)
    s = s - slopes[None, None, :, None, None] * dist
    return jnp.where(dist >= 0, s, -jnp.inf)


def _diff_combine(s, lam):
    p = jax.nn.softmax(s, axis=-1)
    return p[0] - lam * p[1]


def _diff_lambda(lam_params):
    lp = lam_params.astype(jnp.float32)
    return jnp.exp(jnp.sum(lp[0] * lp[1])) - jnp.exp(jnp.sum(lp[2] * lp[3])) + LAMBDA_INIT


def _head_out(o, g):
    y = o * lax.rsqrt(jnp.mean(o * o, axis=-1, keepdims=True) + RMS_EPS)
    y = y * g.astype(jnp.float32) * (1.0 - LAMBDA_INIT)
    return y.reshape(o.shape[0], o.shape[1], ATTN_DIM)


def _mixer_ab(xn, conv_prefix, attend, w_in, conv_w, conv_b, ln_g, ln_b, subln_g, w_out):
    b, t, _ = xn.shape
    z = xn @ w_in
    a_val = z[..., :CONV_A_DIM]
    a_gate = z[..., CONV_A_DIM:2 * CONV_A_DIM]
    o = 2 * CONV_A_DIM
    q = z[..., o:o + ATTN_DIM].reshape(b, t, N_HEADS, 2, HEAD_DIM)
    k = z[..., o + ATTN_DIM:o + 2 * ATTN_DIM].reshape(b, t, N_HEADS, 2, HEAD_DIM)
    v = z[..., o + 2 * ATTN_DIM:].reshape(b, t, N_HEADS, V_DIM)
    g = a_val * jax.nn.sigmoid(a_gate)
    g_hist = jnp.concatenate([conv_prefix.astype(g.dtype), g], axis=1)
    a = _dwconv_valid(g_hist, conv_w) + conv_b.astype(g.dtype)
    a = jax.nn.silu(_layernorm(a, ln_g, ln_b))
    att = _head_out(attend(q, k, v), subln_g).astype(xn.dtype)
    out = jnp.concatenate([a, att], axis=-1) @ w_out
    return out, k.reshape(b, t, N_HEADS, 2 * HEAD_DIM), v, g_hist[:, -(CONV_A_WIDTH - 1):]


def _mixer_c(xn, conv_prefix, w_in, conv_w, w_out):
    z = xn @ w_in
    gate_b = z[..., :CONV_C_DIM]
    gate_c = z[..., CONV_C_DIM:2 * CONV_C_DIM]
    h = z[..., 2 * CONV_C_DIM:]
    u = gate_c * h
    u_hist = jnp.concatenate([conv_prefix.astype(u.dtype), u], axis=1)
    y = gate_b * _dwconv_valid(u_hist, conv_w)
    return y @ w_out, u_hist[:, -(CONV_C_WIDTH - 1):]


def setup_inputs(seed: int = 0) -> dict:
    key = jax.random.key(seed)
    ks = jax.random.split(key, 32)
    f32 = jnp.float32
    n_pages = PAST_LEN // PAGE_SIZE
    n_phys = (DEC_BATCH * n_pages * 5) // 4

    def w(k, shape, fan_in):
        return jax.random.normal(k, shape, f32) * (fan_in ** -0.5)

    def gain(k, shape):
        return 1.0 + 0.02 * jax.random.normal(k, shape, f32)

    page_table = jax.random.permutation(ks[6], n_phys)[:DEC_BATCH * n_pages].reshape(DEC_BATCH, n_pages).astype(jnp.int32)
    return {
        "x_prompt": jax.random.normal(ks[0], (BATCH, SEQ, D_MODEL), f32),
        "x_sample": jax.random.normal(ks[1], (DEC_BATCH, DEC_SEQ, D_MODEL), f32),
        "cache_k": jax.random.normal(ks[2], (n_phys, PAGE_SIZE, N_HEADS, 2 * HEAD_DIM), f32),
        "cache_v": jax.random.normal(ks[3], (n_phys, PAGE_SIZE, N_HEADS, V_DIM), f32),
        "state_conv_a": 0.5 * jax.random.normal(ks[4], (DEC_BATCH, CONV_A_WIDTH - 1, CONV_A_DIM), f32),
        "state_conv_c": jax.random.normal(ks[5], (DEC_BATCH, CONV_C_WIDTH - 1, CONV_C_DIM), f32),
        "page_table": page_table,
        "norm_ffn1": gain(ks[7], (DEPTH, D_MODEL)),
        "ffn1_w_gate": w(ks[8], (DEPTH, D_MODEL, D_FF), D_MODEL),
        "ffn1_w_up": w(ks[9], (DEPTH, D_MODEL, D_FF), D_MODEL),
        "ffn1_w_down": w(ks[10], (DEPTH, D_FF, D_MODEL), D_FF),
        "norm_mix": gain(ks[11], (DEPTH, D_MODEL)),
        "norm_ffn2": gain(ks[12], (DEPTH, D_MODEL)),
        "ffn2_w_gate": w(ks[13], (DEPTH, D_MODEL, D_FF), D_MODEL),
        "ffn2_w_up": w(ks[14], (DEPTH, D_MODEL, D_FF), D_MODEL),
        "ffn2_w_down": w(ks[15], (DEPTH, D_FF, D_MODEL), D_FF),
        "w_in_ab": w(ks[16], (D_MODEL, W_IN_AB_COLS), D_MODEL),
        "conv_a_w": w(ks[17], (CONV_A_WIDTH, CONV_A_DIM), CONV_A_WIDTH),
        "conv_a_b": 0.02 * jax.random.normal(ks[18], (CONV_A_DIM,), f32),
        "conv_a_ln_g": gain(ks[19], (CONV_A_DIM,)),
        "conv_a_ln_b": 0.02 * jax.random.normal(ks[20], (CONV_A_DIM,), f32),
        "diff_lambda": 0.1 * jax.random.normal(ks[21], (4, HEAD_DIM), f32),
        "diff_subln_g": gain(ks[22], (V_DIM,)),
        "w_out_ab": w(ks[23], (CONV_A_DIM + ATTN_DIM, D_MODEL), CONV_A_DIM + ATTN_DIM),
        "w_in_c": w(ks[24], (D_MODEL, 3 * CONV_C_DIM), D_MODEL),
        "conv_c_w": w(ks[25], (CONV_C_WIDTH, CONV_C_DIM), CONV_C_WIDTH),
        "w_out_c": w(ks[26], (CONV_C_DIM, D_MODEL), CONV_C_DIM),
        "final_norm": gain(ks[27], (D_MODEL,)),
    }


def reference(x_prompt, x_sample, cache_k, cache_v, state_conv_a, state_conv_c, page_table,
              norm_ffn1, ffn1_w_gate, ffn1_w_up, ffn1_w_down, norm_mix, norm_ffn2,
              ffn2_w_gate, ffn2_w_up, ffn2_w_down, w_in_ab, conv_a_w, conv_a_b, conv_a_ln_g,
              conv_a_ln_b, diff_lambda, diff_subln_g, w_out_ab, w_in_c, conv_c_w, w_out_c, final_norm):
    slopes = _alibi_slopes()
    lam = _diff_lambda(diff_lambda)

    def attend_prompt(q, k, v):
        b, t = q.shape[0], q.shape[1]
        nb = t // Q_BLOCK
        k_pos = jnp.arange(t, dtype=jnp.int32)
        q_blocks = jnp.moveaxis(q.reshape(b, nb, Q_BLOCK, N_HEADS, 2, HEAD_DIM), 1, 0)

        def one_block(args):
            q_blk, i = args
            q_pos = i * Q_BLOCK + jnp.arange(Q_BLOCK, dtype=jnp.int32)
            a = _diff_combine(_diff_scores(q_blk, k, q_pos, k_pos, slopes), lam)
            return jnp.einsum('bhqk,bkhd->bqhd', a.astype(v.dtype), v, preferred_element_type=jnp.float32)

        o = lax.map(one_block, (q_blocks, jnp.arange(nb, dtype=jnp.int32)))
        return jnp.moveaxis(o, 0, 1).reshape(b, t, N_HEADS, V_DIM)

    def attend_sample(q, k, v):
        b, s = q.shape[0], q.shape[1]
        past = page_table.shape[1] * cache_k.shape[1]
        k_past = cache_k[page_table].reshape(b, past, N_HEADS, 2, HEAD_DIM)
        v_past = cache_v[page_table].reshape(b, past, N_HEADS, V_DIM)
        q_pos = past + jnp.arange(s, dtype=jnp.int32)
        scores = jnp.concatenate([
            _diff_scores(q, k_past, q_pos, jnp.arange(past, dtype=jnp.int32), slopes),
            _diff_scores(q, k, q_pos, q_pos, slopes)], axis=-1)
        a = _diff_combine(scores, lam)
        o_past = jnp.einsum('bhqk,bkhd->bqhd', a[..., :past].astype(v_past.dtype), v_past, preferred_element_type=jnp.float32)
        o_new = jnp.einsum('bhqk,bkhd->bqhd', a[..., past:].astype(v.dtype), v, preferred_element_type=jnp.float32)
        return o_past + o_new

    def forward(x, attend, conv_a_prefix, conv_c_prefix):
        k_rows = v_rows = conv_a_state = conv_c_state = None
        for layer in range(DEPTH):
            x = x + 0.5 * _swiglu(_rmsnorm(x, norm_ffn1[layer]), ffn1_w_gate[layer], ffn1_w_up[layer], ffn1_w_down[layer])
            xn = _rmsnorm(x, norm_mix[layer])
            if layer % 2 == 0:
                m, k_rows, v_rows, conv_a_state = _mixer_ab(xn, conv_a_prefix, attend, w_in_ab, conv_a_w, conv_a_b,
                                                            conv_a_ln_g, conv_a_ln_b, diff_subln_g, w_out_ab)
            else:
                m, conv_c_state = _mixer_c(xn, conv_c_prefix, w_in_c, conv_c_w, w_out_c)
            x = x + m
            x = x + 0.5 * _swiglu(_rmsnorm(x, norm_ffn2[layer]), ffn2_w_gate[layer], ffn2_w_up[layer], ffn2_w_down[layer])
        return _rmsnorm(x, final_norm), k_rows, v_rows, conv_a_state, conv_c_state

    bp = x_prompt.shape[0]
    zeros_a = jnp.zeros((bp, CONV_A_WIDTH - 1, CONV_A_DIM), x_prompt.dtype)
    zeros_c = jnp.zeros((bp, CONV_C_WIDTH - 1, CONV_C_DIM), x_prompt.dtype)
    y_prompt, k_prompt, v_prompt, conv_a_prompt, conv_c_prompt = forward(x_prompt, attend_prompt, zeros_a, zeros_c)
    y_sample, k_sample, v_sample, conv_a_sample, conv_c_sample = forward(x_sample, attend_sample, state_conv_a, state_conv_c)
    return (y_prompt, y_sample, k_prompt, v_prompt, conv_a_prompt, conv_c_prompt,
            k_sample, v_sample, conv_a_sample, conv_c_sample)
```

```python
import math
import numpy as np
import concourse.bass as bass
import concourse.mybir as mybir
from concourse.bass_utils import run_bass_kernel_spmd

F32 = mybir.dt.float32
BF16 = mybir.dt.bfloat16
I32 = mybir.dt.int32
AF = mybir.ActivationFunctionType
ALU = mybir.AluOpType

NCORES = 8
D = 2048
DC = 16
DFF = 5632
SEQ = 2048
NH = 8
PAST = 8192
NPG = 64
NPHYS = 2560
LAMBDA_INIT = 0.8 - 0.6 * math.exp(-0.3 * 0)
RMS_EPS = 1e-6
LN_EPS = 1e-5
BIG = 1.0e9

CV = {}
_o = 0
for _nm, _n in [("nf1_0", 16), ("nf1_1", 16), ("nm_0", 16), ("nm_1", 16), ("nf2_0", 16), ("nf2_1", 16),
                ("fin", 16), ("caw", 8 * 31), ("cab", 8), ("lng", 8), ("lnb", 8), ("ccw", 16 * 3),
                ("subg", 1), ("dl", 4)]:
    CV[_nm] = _o
    _o += _n
NCV = _o

ENGS = ("pe", "act", "dve", "pool", "sp")
_NC_CACHE = {}


class Prog:
    def __init__(self, nc):
        self.nc = nc
        self.ops = {e: [] for e in ENGS}
        self.cnt = {e: 0 for e in ENGS}
        self.seen = {e: {} for e in ENGS}
        self.dcnt = {}
        self.lastw = {}
        self.readers = {}
        self.barcnt = 0
        self.epoch = 0
        self.maxcnt = 0

    def _deps(self, r, w):
        deps = []
        for b in r:
            t = self.lastw.get(b)
            if t is not None:
                deps.append(t)
            if b.startswith("ps"):
                deps.extend(self.readers.get(b, ()))
        for b in w:
            t = self.lastw.get(b)
            if t is not None:
                deps.append(t)
            deps.extend(self.readers.get(b, ()))
        return deps

    def _waits(self, eng, deps):
        for d in deps:
            if d is None:
                continue
            if self.seen[eng].get(d[0], 0) >= d[1]:
                continue
            self.seen[eng][d[0]] = d[1]
            self.ops[eng].append(("wait", d[0], d[1]))

    def _reg(self, tok, r, w):
        for b in w:
            self.lastw[b] = tok
            self.readers[b] = []
        for b in r:
            if b not in w:
                self.readers.setdefault(b, []).append(tok)

    def group(self, eng, fns, r=(), w=()):
        self._waits(eng, self._deps(r, w))
        ekey = f"{eng}@{self.epoch}"
        for i, fn in enumerate(fns):
            self.ops[eng].append(("op", fn, i == len(fns) - 1, ekey))
        self.cnt[eng] += 1
        self.maxcnt = max(self.maxcnt, self.cnt[eng])
        tok = (ekey, self.cnt[eng])
        self._reg(tok, r, w)
        return tok

    def op(self, eng, fn, r=(), w=()):
        return self.group(eng, [fn], r, w)

    def dma(self, q, out, in_, key, r=(), w=()):
        self._waits(q, self._deps(r, w))
        self.dcnt[key] = self.dcnt.get(key, 0) + 16
        self.ops[q].append(("dma", out, in_, key))
        tok = ("D:" + key, self.dcnt[key])
        self._reg(tok, r, w)
        return tok

    def gather(self, out, in_, idx, key, r=(), w=()):
        self._waits("pool", self._deps(r, w))
        self.dcnt[key] = self.dcnt.get(key, 0) + 16
        self.ops["pool"].append(("gather", out, in_, idx, key))
        tok = ("D:" + key, self.dcnt[key])
        self._reg(tok, r, w)
        return tok

    def barrier(self, new_epoch=False):
        deps = [(f"{e}@{self.epoch}", self.cnt[e]) for e in ENGS if e != "sp" and self.cnt[e] > 0]
        deps += [("D:" + k, v) for k, v in self.dcnt.items()]
        self._waits("sp", deps)
        self.barcnt += 1
        self.ops["sp"].append(("barinc",))
        for e in ENGS:
            if e != "sp":
                self.ops[e].append(("wait", "BAR", self.barcnt))
        self.lastw.clear()
        self.readers.clear()
        if new_epoch:
            self.epoch += 1
            for e in ENGS:
                self.cnt[e] = 0

    def finish(self):
        deps = [(f"{e}@{self.epoch}", self.cnt[e]) for e in ENGS if e != "sp" and self.cnt[e] > 0]
        deps += [("D:" + k, v) for k, v in self.dcnt.items()]
        self._waits("sp", deps)

    def emit(self, stack):
        nc = self.nc
        sems = {}
        keys = ["BAR"] + ["D:" + k for k in self.dcnt]
        for ep in range(self.epoch + 1):
            keys += [f"{e}@{ep}" for e in ("pe", "act", "dve", "pool")]
        for k in keys:
            sems[k] = stack.enter_context(nc.semaphore("s_" + k.replace(":", "_").replace(".", "_").replace("@", "_e")))
        block = stack.enter_context(nc.Block())

        def replay(name, e):
            for it in self.ops[name]:
                kind = it[0]
                if kind == "wait":
                    e.wait_ge(sems[it[1]], it[2])
                elif kind == "op":
                    ins = it[1](e)
                    if it[2]:
                        ins.then_inc(sems[it[3]], 1)
                elif kind == "dma":
                    e.dma_start(out=it[1], in_=it[2]).then_inc(sems["D:" + it[3]], 16)
                elif kind == "gather":
                    e.indirect_dma_start(out=it[1], out_offset=None, in_=it[2],
                                         in_offset=bass.IndirectOffsetOnAxis(ap=it[3], axis=0)
                                         ).then_inc(sems["D:" + it[4]], 16)
                elif kind == "barinc":
                    e.sem_inc(sems["BAR"], 1)

        @block.tensor
        def _(e):
            replay("pe", e)

        @block.scalar
        def _(e):
            replay("act", e)

        @block.vector
        def _(e):
            replay("dve", e)

        @block.gpsimd
        def _(e):
            replay("pool", e)

        @block.sync
        def _(e):
            replay("sp", e)


def build_program(nseq=1, nsp=1, dbg=None):
    from contextlib import ExitStack
    nc = bass.Bass("TRN2", target_bir_lowering=False)
    dr = {}

    def din(name, shape, dt=F32):
        dr[name] = nc.dram_tensor(name, list(shape), dt, kind="ExternalInput").ap()

    def dout(name, shape):
        dr[name] = nc.dram_tensor(name, list(shape), F32, kind="ExternalOutput").ap()

    din("x_p", [nseq * SEQ, D]); din("x_s", [max(nsp, 1) * 16, D])
    crow = NPHYS * 128 if nsp > 0 else 128
    nsp_ = max(nsp, 1)
    din("cache_k", [crow, 1024]); din("cache_v", [crow, 1024])
    din("sca", [nsp_ * 4, 30, 1024]); din("scc", [nsp_ * 4, 2, 2048]); din("pt", [1, nsp_ * 256], I32)
    for nm in ("ffn1_w_gate", "ffn1_w_up", "ffn2_w_gate", "ffn2_w_up"):
        din(nm, [2, D, DFF])
    for nm in ("ffn1_w_down", "ffn2_w_down"):
        din(nm, [2, DFF, D])
    din("w_in_ab", [D, 5120]); din("w_out_ab", [D, D]); din("w_in_c", [D, 6144]); din("w_out_c", [D, D])
    din("cvec", [128, NCV])
    dout("y_p", [nseq * SEQ, D]); dout("y_s", [max(nsp, 1) * 16, D])
    dout("k_p", [nseq * SEQ, 1024]); dout("v_p", [nseq * SEQ, 1024])
    dout("ca_p", [nseq * 30, 1024]); dout("cc_p", [nseq * 2, 2048])
    dout("k_s", [max(nsp, 1) * 16, 1024]); dout("v_s", [max(nsp, 1) * 16, 1024])
    dout("ca_s", [max(nsp, 1) * 4, 30, 1024]); dout("cc_s", [max(nsp, 1) * 4, 2, 2048])
    if dbg:
        dout("dbg", [128, 16 * 1024])

    stack = ExitStack()
    ARENA_BYTES = 204 * 1024
    arena = stack.enter_context(nc.sbuf_tensor("arena", [128, ARENA_BYTES // 4], F32))
    ps = [stack.enter_context(nc.psum_tensor(f"ps{i}", [128, 512], F32)) for i in range(8)]

    def view(off, dt, shape):
        n = 1
        for s in shape:
            n *= s
        es = 4 if dt in (F32, I32) else 2
        assert off % 4 == 0 and (n * es) % 4 == 0
        assert off + n * es <= ARENA_BYTES, (off, n * es)
        a = arena[:, off // 4:(off + n * es) // 4]
        if dt != F32:
            a = a.bitcast(dt)
        if len(shape) == 2:
            a = a.rearrange("p (a b) -> p a b", a=shape[0])
        elif len(shape) == 3:
            a = a.rearrange("p (a b c) -> p a b c", a=shape[0], b=shape[1])
        return a

    P = Prog(nc)

    X_OFF = 0
    XN_OFF = 65536
    R_OFF = 98304
    R_SIZE = 77824
    C_OFF = R_OFF + R_SIZE
    co = [C_OFF]

    def calloc(nbytes):
        o = co[0]
        co[0] += (nbytes + 3) // 4 * 4
        assert co[0] <= ARENA_BYTES, co[0]
        return o

    cvec = view(calloc(NCV * 4), F32, [NCV])
    ident_f = view(calloc(512), F32, [128])
    ident_b = view(calloc(256), BF16, [128])
    ones_f = view(calloc(512), F32, [128])
    ones_b = view(calloc(256), BF16, [128])
    neglam = view(calloc(4), F32, [1])
    lamtmp = view(calloc(32), F32, [8])
    ghist = view(calloc(8 * 30 * 4), F32, [8, 30])
    uhist = view(calloc(16 * 2 * 4), F32, [16, 2])
    dtile = [view(calloc(2048), F32, [512]) for _ in range(5)]
    iota_i = view(calloc(2048), I32, [512])
    dtile_tmp = view(R_OFF, F32, [512])

    P.dma("sp", cvec, dr["cvec"], "cvec", w=["cvec"])
    P.op("pool", lambda e: e.iota(iota_i[:, 0:128], pattern=[[1, 128]], base=0, channel_multiplier=-1), w=["iota"])
    P.op("dve", lambda e: e.tensor_copy(out=ident_f, in_=iota_i[:, 0:128]), r=["iota"], w=["identf"])
    P.op("dve", lambda e: e.tensor_single_scalar(out=ident_f, in_=ident_f, scalar=0.0, op=ALU.is_equal), w=["identf"])
    P.op("dve", lambda e: e.tensor_copy(out=ident_b, in_=ident_f), r=["identf"], w=["identb"])
    P.op("dve", lambda e: e.memset(ones_f, 1.0), w=["onesf"])
    P.op("dve", lambda e: e.memset(ones_b, 1.0), w=["onesb"])
    P.op("pool", lambda e: e.iota(iota_i, pattern=[[1, 512]], base=0, channel_multiplier=-1), r=["iota"], w=["iota"])
    P.op("dve", lambda e: e.tensor_copy(out=dtile[0], in_=iota_i), r=["iota"], w=["d0"])
    for jj in range(4):
        dm = dtile[1 + jj]
        P.op("dve", lambda e, dm=dm, jj=jj: e.tensor_scalar_add(out=dm, in0=dtile[0], scalar1=float(-128 * jj)),
             r=["d0"], w=[f"dm{jj}"])
        P.op("dve", lambda e, dm=dm: e.tensor_scalar(out=dtile_tmp, in0=dm, scalar1=0.0, scalar2=BIG,
                                                     op0=ALU.is_lt, op1=ALU.mult), r=[f"dm{jj}"], w=["dtmp"])
        P.op("dve", lambda e, dm=dm: e.tensor_add(out=dm, in0=dm, in1=dtile_tmp), r=["dtmp"], w=[f"dm{jj}"])
    dl = CV["dl"]
    P.op("dve", lambda e: e.tensor_mul(out=lamtmp[:, 0:1], in0=cvec[:, dl:dl + 1], in1=cvec[:, dl + 1:dl + 2]),
         r=["cvec"], w=["lamtmp"])
    P.op("dve", lambda e: e.tensor_mul(out=lamtmp[:, 1:2], in0=cvec[:, dl + 2:dl + 3], in1=cvec[:, dl + 3:dl + 4]),
         r=["cvec"], w=["lamtmp"])
    P.op("pe", lambda e: e.matmul(ps[7][:, 0:2], lhsT=ones_f, rhs=lamtmp[:, 0:2], start=True, stop=True),
         r=["lamtmp", "onesf"], w=["ps7"])
    P.op("act", lambda e: e.activation(out=lamtmp[:, 2:4], in_=ps[7][:, 0:2], func=AF.Exp), r=["ps7"], w=["lamtmp"])
    P.op("dve", lambda e: e.tensor_sub(out=lamtmp[:, 4:5], in0=lamtmp[:, 3:4], in1=lamtmp[:, 2:3]),
         r=["lamtmp"], w=["lamtmp"])
    P.op("dve", lambda e: e.tensor_scalar_add(out=neglam, in0=lamtmp[:, 4:5], scalar1=-LAMBDA_INIT), r=["lamtmp"], w=["neglam"])
    P.op("act", lambda e: e.memzero(ghist.rearrange("p a b -> p (a b)")), w=["ghist"])
    P.op("act", lambda e: e.memzero(uhist.rearrange("p a b -> p (a b)")), w=["uhist"])

    P.barrier()
    st = {"dbg": dbg, "nseq": nseq, "nsp": nsp, "mixer": True}
    build_passes(nc, P, dr, view, ps, st, dict(
        X_OFF=X_OFF, XN_OFF=XN_OFF, R_OFF=R_OFF, R_SIZE=R_SIZE, cvec=cvec, ident_f=ident_f, ident_b=ident_b,
        ones_f=ones_f, ones_b=ones_b, neglam=neglam, ghist=ghist, uhist=uhist, dtile=dtile))
    P.finish()
    assert P.maxcnt < 60000, P.maxcnt
    _NC_CACHE["maxcnt"] = P.maxcnt
    P.emit(stack)
    stack.close()
    return nc


def build_passes(nc, P, dr, view, ps, st, K):
    X_OFF, XN_OFF, R_OFF = K["X_OFF"], K["XN_OFF"], K["R_OFF"]
    cvec, ident_f, ident_b, ones_f, ones_b = K["cvec"], K["ident_f"], K["ident_b"], K["ones_f"], K["ones_b"]
    neglam, ghist, uhist, dtile = K["neglam"], K["ghist"], K["uhist"], K["dtile"]

    def cvc(name, i=0, n=1):
        o = CV[name] + i
        return cvec[:, o:o + n]

    def one_pass(pname, SEQI=0, SI=0):
        ROW0 = SI * 16 if pname == "S" else SEQI * SEQ
        T = 16 if pname == "S" else 1024
        tiles = [(0, 16)] if pname == "S" else [(0, 512), (512, 512)]
        B0 = {"A": 0, "B": 1024, "S": 0}[pname]
        x = view(X_OFF, F32, [16, T])
        xn = view(XN_OFF, BF16, [16, T])
        RN = R_OFF + 69632
        rnacc = view(RN, F32, [512])
        rnsq = [view(RN + 2048, F32, [512]), view(RN + 4096, F32, [512])]
        rstd = view(RN + 6144, F32, [512])

        def load_x():
            stg = [view(R_OFF, F32, [2048]), view(R_OFF + 8192, F32, [2048])]
            if pname == "S":
                blocks = [(0, 16)]
                src = dr["x_s"]
            else:
                blocks = [(i * 128, 128) for i in range(8)]
                src = dr["x_p"]
            for bi, (t0, n) in enumerate(blocks):
                sg = stg[bi % 2]
                P.dma("sp", sg[0:n, :], src[ROW0 + B0 + t0:ROW0 + B0 + t0 + n, :], f"stg{bi % 2}", w=[f"stg{bi % 2}"])
                for q in range(4):
                    bank = ps[(bi * 4 + q) % 4]
                    bname = f"ps{(bi * 4 + q) % 4}"
                    fns = []
                    for j in range(4):
                        c = q * 4 + j
                        fns.append(lambda e, bank=bank, sg=sg, c=c, j=j, n=n: e.transpose(
                            bank[:, j * 128:j * 128 + n], sg[0:n, c * 128:(c + 1) * 128], ident_f[0:n, 0:n]))
                    P.group("pe", fns, r=[f"stg{bi % 2}", "identf"], w=[bname])
                    eng = "act" if q % 2 == 0 else "dve"
                    outv = x[:, q * 4:q * 4 + 4, t0:t0 + n]
                    inv = bank[:, :].rearrange("p (a b) -> p a b", a=4)[:, :, 0:n]
                    if eng == "act":
                        P.op("act", lambda e, outv=outv, inv=inv: e.copy(out=outv, in_=inv), r=[bname], w=["x"])
                    else:
                        P.op("dve", lambda e, outv=outv, inv=inv: e.tensor_copy(out=outv, in_=inv), r=[bname], w=["x"])

        def rmsnorm(gname):
            for ti, (t0, n) in enumerate(tiles):
                for c in range(16):
                    sq = rnsq[c % 2][:, 0:n]
                    P.op("act", lambda e, sq=sq, c=c, t0=t0, n=n: e.activation(
                        out=sq, in_=x[:, c, t0:t0 + n], func=AF.Square), r=["x"], w=[f"rnsq{c % 2}"])
                    if c == 0:
                        P.op("dve", lambda e, sq=sq, n=n: e.tensor_copy(out=rnacc[:, 0:n], in_=sq),
                             r=["rnsq0"], w=["rnacc"])
                    else:
                        P.op("dve", lambda e, sq=sq, n=n: e.tensor_add(out=rnacc[:, 0:n], in0=rnacc[:, 0:n], in1=sq),
                             r=[f"rnsq{c % 2}"], w=["rnacc"])
                P.op("pe", lambda e, n=n: e.matmul(ps[6][:, 0:n], lhsT=ones_f, rhs=rnacc[:, 0:n], start=True, stop=True),
                     r=["rnacc", "onesf"], w=["ps6"])
                P.op("dve", lambda e, n=n: e.tensor_scalar(out=rstd[:, 0:n], in0=ps[6][:, 0:n], scalar1=1.0 / D,
                                                           scalar2=RMS_EPS, op0=ALU.mult, op1=ALU.add),
                     r=["ps6"], w=["rstd"])
                P.op("act", lambda e, n=n: e.sqrt(out=rstd[:, 0:n], in_=rstd[:, 0:n]), r=["rstd"], w=["rstd"])
                P.op("dve", lambda e, n=n: e.reciprocal(out=rstd[:, 0:n], in_=rstd[:, 0:n]), r=["rstd"], w=["rstd"])
                for c in range(16):
                    P.op("dve", lambda e, c=c, t0=t0, n=n: e.scalar_tensor_tensor(
                        out=xn[:, c, t0:t0 + n], in0=x[:, c, t0:t0 + n], scalar=cvc(gname, c), in1=rstd[:, 0:n],
                        op0=ALU.mult, op1=ALU.mult), r=["x", "rstd", "cvec"], w=[f"xn{ti}"])

        def ffn(layer, which):
            rmsnorm(f"nf{which}_{layer}")
            wg = dr[f"ffn{which}_w_gate"][layer].rearrange("(kc p) f -> p kc f", p=128)
            wu = dr[f"ffn{which}_w_up"][layer].rearrange("(kc p) f -> p kc f", p=128)
            wd = dr[f"ffn{which}_w_down"][layer]
            WG = [view(R_OFF + s * 24576, BF16, [16, 256]) for s in range(2)]
            WU = [view(R_OFF + s * 24576 + 8192, BF16, [16, 256]) for s in range(2)]
            WD = [view(R_OFF + s * 24576 + 16384, BF16, [2, 2048]) for s in range(2)]
            H = [view(R_OFF + 49152 + s * 4096, BF16, [2, 1024]) for s in range(2)]
            SG = [view(R_OFF + 57344 + s * 2048, F32, [512]) for s in range(2)]
            NG = DFF // 256

            def load(g):
                s = g % 2
                P.dma("pool", WG[s], wg[:, :, g * 256:(g + 1) * 256], f"wg{s}", w=[f"wg{s}"])
                P.dma("pool", WU[s], wu[:, :, g * 256:(g + 1) * 256], f"wu{s}", w=[f"wu{s}"])
                P.dma("pool", WD[s], wd[g * 256:(g + 1) * 256, :].rearrange("(j p) d -> p j d", p=128),
                      f"wd{s}", w=[f"wd{s}"])

            stepc = [0]
            dnc = [0]

            def gateup(g):
                s = g % 2
                for fi in range(2):
                    for ti, (t0, n) in enumerate(tiles):
                        k = stepc[0] % 2
                        stepc[0] += 1
                        gb, ub = ps[2 * k], ps[2 * k + 1]
                        P.group("pe", [lambda e, gb=gb, s=s, kc=kc, fi=fi, t0=t0, n=n: e.matmul(
                            gb[:, 0:n], lhsT=WG[s][:, kc, fi * 128:(fi + 1) * 128], rhs=xn[:, kc, t0:t0 + n],
                            start=(kc == 0), stop=(kc == 15)) for kc in range(16)],
                            r=[f"wg{s}", f"xn{ti}"], w=[f"ps{2 * k}"])
                        P.group("pe", [lambda e, ub=ub, s=s, kc=kc, fi=fi, t0=t0, n=n: e.matmul(
                            ub[:, 0:n], lhsT=WU[s][:, kc, fi * 128:(fi + 1) * 128], rhs=xn[:, kc, t0:t0 + n],
                            start=(kc == 0), stop=(kc == 15)) for kc in range(16)],
                            r=[f"wu{s}", f"xn{ti}"], w=[f"ps{2 * k + 1}"])
                        P.op("act", lambda e, gb=gb, k=k, n=n: e.activation(out=SG[k][:, 0:n], in_=gb[:, 0:n], func=AF.Silu),
                             r=[f"ps{2 * k}"], w=[f"sg{k}"])
                        P.op("dve", lambda e, ub=ub, k=k, s=s, fi=fi, t0=t0, n=n: e.tensor_tensor(
                            out=H[s][:, fi, t0:t0 + n], in0=SG[k][:, 0:n], in1=ub[:, 0:n], op=ALU.mult),
                            r=[f"sg{k}", f"ps{2 * k + 1}"], w=[f"h{s}"])

            def down(g):
                s = g % 2
                for dc in range(16):
                    for ti, (t0, n) in enumerate(tiles):
                        k = 4 + dnc[0] % 2
                        dnc[0] += 1
                        bank = ps[k]
                        P.group("pe", [lambda e, bank=bank, s=s, j=j, dc=dc, t0=t0, n=n: e.matmul(
                            bank[:, 0:n], lhsT=WD[s][:, j, dc * 128:(dc + 1) * 128], rhs=H[s][:, j, t0:t0 + n],
                            start=(j == 0), stop=(j == 1)) for j in range(2)],
                            r=[f"wd{s}", f"h{s}"], w=[f"ps{k}"])
                        P.op("dve", lambda e, bank=bank, dc=dc, t0=t0, n=n: e.scalar_tensor_tensor(
                            out=x[:, dc, t0:t0 + n], in0=bank[:, 0:n], scalar=0.5, in1=x[:, dc, t0:t0 + n],
                            op0=ALU.mult, op1=ALU.add), r=[f"ps{k}"], w=["x"])

            load(0)
            load(1)
            for g in range(NG):
                gateup(g)
                if g >= 1:
                    down(g - 1)
                    if g + 1 < NG:
                        load(g + 1)
            down(NG - 1)

        def final_store():
            yf = view(R_OFF, F32, [16, 128])
            stg = [view(R_OFF + 8192, F32, [2048]), view(R_OFF + 16384, F32, [2048])]
            dst = dr["y_s"] if pname == "S" else dr["y_p"]
            bi = 0
            for ti, (t0, n) in enumerate(tiles):
                for c in range(16):
                    sq = rnsq[c % 2][:, 0:n]
                    P.op("act", lambda e, sq=sq, c=c, t0=t0, n=n: e.activation(
                        out=sq, in_=x[:, c, t0:t0 + n], func=AF.Square), r=["x"], w=[f"rnsq{c % 2}"])
                    if c == 0:
                        P.op("dve", lambda e, sq=sq, n=n: e.tensor_copy(out=rnacc[:, 0:n], in_=sq),
                             r=["rnsq0"], w=["rnacc"])
                    else:
                        P.op("dve", lambda e, sq=sq, n=n: e.tensor_add(out=rnacc[:, 0:n], in0=rnacc[:, 0:n], in1=sq),
                             r=[f"rnsq{c % 2}"], w=["rnacc"])
                P.op("pe", lambda e, n=n: e.matmul(ps[6][:, 0:n], lhsT=ones_f, rhs=rnacc[:, 0:n], start=True, stop=True),
                     r=["rnacc", "onesf"], w=["ps6"])
                P.op("dve", lambda e, n=n: e.tensor_scalar(out=rstd[:, 0:n], in0=ps[6][:, 0:n], scalar1=1.0 / D,
                                                           scalar2=RMS_EPS, op0=ALU.mult, op1=ALU.add),
                     r=["ps6"], w=["rstd"])
                P.op("act", lambda e, n=n: e.sqrt(out=rstd[:, 0:n], in_=rstd[:, 0:n]), r=["rstd"], w=["rstd"])
                P.op("dve", lambda e, n=n: e.reciprocal(out=rstd[:, 0:n], in_=rstd[:, 0:n]), r=["rstd"], w=["rstd"])
                nb = max(1, n // 128)
                for b_ in range(nb):
                    bn = min(128, n)
                    c0 = b_ * 128
                    sgt = stg[bi % 2]
                    for c in range(16):
                        P.op("dve", lambda e, c=c, t0=t0, c0=c0, bn=bn: e.scalar_tensor_tensor(
                            out=yf[:, c, 0:bn], in0=x[:, c, t0 + c0:t0 + c0 + bn], scalar=cvc("fin", c),
                            in1=rstd[:, c0:c0 + bn], op0=ALU.mult, op1=ALU.mult), r=["x", "rstd", "cvec"], w=["yf"])
                    for q in range(4):
                        bank = ps[q]
                        P.group("pe", [lambda e, bank=bank, q=q, j=j, bn=bn: e.transpose(
                            bank[0:bn, j * 128:(j + 1) * 128], yf[:, q * 4 + j, 0:bn], ident_f) for j in range(4)],
                            r=["yf", "identf"], w=[f"ps{q}"])
                        if q % 2 == 0:
                            P.op("act", lambda e, bank=bank, sgt=sgt, q=q, bn=bn: e.copy(
                                out=sgt[0:bn, q * 512:(q + 1) * 512], in_=bank[0:bn, :]), r=[f"ps{q}"], w=[f"ystg{bi % 2}"])
                        else:
                            P.op("dve", lambda e, bank=bank, sgt=sgt, q=q, bn=bn: e.tensor_copy(
                                out=sgt[0:bn, q * 512:(q + 1) * 512], in_=bank[0:bn, :]), r=[f"ps{q}"], w=[f"ystg{bi % 2}"])
                    P.dma("sp", dst[ROW0 + B0 + t0 + c0:ROW0 + B0 + t0 + c0 + bn, :], sgt[0:bn, :], f"ystg{bi % 2}", r=[f"ystg{bi % 2}"])
                    bi += 1


        SLOPES = [2.0 ** (-(h + 1)) for h in range(NH)]
        WMs = [view(R_OFF + i * 4096, BF16, [16, 128]) for i in range(6)]
        AB = view(R_OFF + 24576, BF16, [8, T])
        U0 = R_OFF + 40960

        def wload(slot, src_cols):
            P.dma("pool", WMs[slot], src_cols, f"wm{slot}", w=[f"wm{slot}"])

        def proj(slot, bankidx, t0, n, ti):
            bank = ps[bankidx]
            P.group("pe", [lambda e, bank=bank, slot=slot, kc=kc, t0=t0, n=n: e.matmul(
                bank[:, 0:n], lhsT=WMs[slot][:, kc, :], rhs=xn[:, kc, t0:t0 + n], start=(kc == 0), stop=(kc == 15))
                for kc in range(16)], r=[f"wm{slot}", f"xn{ti}"], w=[f"ps{bankidx}"])
            return bank

        def combine(U1, U2, Z1, Z2, outap, n, shp, rbufs):
            o1, o2, o3 = rbufs
            def v(a):
                return a if shp is None else a.rearrange("p (a b) -> p a b", a=shp[0])
            P.op("dve", lambda e: e.reciprocal(out=v(o1[:, 0:n]), in_=Z1), r=["ps3"], w=["o1"])
            P.op("dve", lambda e: e.tensor_tensor(out=v(o1[:, 0:n]), in0=v(o1[:, 0:n]), in1=U1, op=ALU.mult),
                 r=["o1", "ps5"], w=["o1"])
            P.op("dve", lambda e: e.reciprocal(out=v(o2[:, 0:n]), in_=Z2), r=["ps4"], w=["o2"])
            P.op("dve", lambda e: e.tensor_tensor(out=v(o2[:, 0:n]), in0=v(o2[:, 0:n]), in1=U2, op=ALU.mult),
                 r=["o2", "ps6"], w=["o2"])
            P.op("dve", lambda e: e.scalar_tensor_tensor(out=o1[:, 0:n], in0=o2[:, 0:n], scalar=neglam[:, 0:1],
                                                         in1=o1[:, 0:n], op0=ALU.mult, op1=ALU.add),
                 r=["o1", "o2", "neglam"], w=["o1"])
            P.op("act", lambda e: e.activation(out=o2[:, 0:n], in_=o1[:, 0:n], func=AF.Square), r=["o1"], w=["o2"])
            P.op("pe", lambda e: e.matmul(ps[7][:, 0:n], lhsT=ones_f, rhs=o2[:, 0:n], start=True, stop=True),
                 r=["o2", "onesf"], w=["ps7"])
            P.op("dve", lambda e: e.tensor_scalar(out=o3[:, 0:n], in0=ps[7][:, 0:n], scalar1=1.0 / 128, scalar2=RMS_EPS,
                                                  op0=ALU.mult, op1=ALU.add), r=["ps7"], w=["o3"])
            P.op("act", lambda e: e.sqrt(out=o3[:, 0:n], in_=o3[:, 0:n]), r=["o3"], w=["o3"])
            P.op("dve", lambda e: e.reciprocal(out=o3[:, 0:n], in_=o3[:, 0:n]), r=["o3"], w=["o3"])
            P.op("dve", lambda e: e.tensor_scalar(out=o1[:, 0:n], in0=o1[:, 0:n], scalar1=cvc("subg"),
                                                  scalar2=1.0 - LAMBDA_INIT, op0=ALU.mult, op1=ALU.mult),
                 r=["o1", "cvec"], w=["o1"])
            P.op("dve", lambda e: e.tensor_tensor(out=outap, in0=v(o1[:, 0:n]), in1=v(o3[:, 0:n]), op=ALU.mult),
                 r=["o1", "o3"], w=["att"])

        def out_proj(wsrc, nchunks, src_tile, wname):
            wv = wsrc.rearrange("(kc p) f -> p kc f", p=128)
            cnt = 0
            for dc in range(16):
                slot = dc % 6
                P.dma("pool", WMs[slot][:, 0:nchunks, :], wv[:, :, dc * 128:(dc + 1) * 128], f"wm{slot}", w=[f"wm{slot}"])
                for ti, (t0, n) in enumerate(tiles):
                    k = cnt % 2
                    cnt += 1
                    bank = ps[k]
                    P.group("pe", [lambda e, bank=bank, slot=slot, kc=kc, t0=t0, n=n: e.matmul(
                        bank[:, 0:n], lhsT=WMs[slot][:, kc, :], rhs=src_tile[:, kc, t0:t0 + n],
                        start=(kc == 0), stop=(kc == nchunks - 1)) for kc in range(nchunks)],
                        r=[f"wm{slot}", wname], w=[f"ps{k}"])
                    P.op("dve", lambda e, bank=bank, dc=dc, t0=t0, n=n: e.tensor_tensor(
                        out=x[:, dc, t0:t0 + n], in0=bank[:, 0:n], in1=x[:, dc, t0:t0 + n], op=ALU.add),
                        r=[f"ps{k}"], w=["x"])

        def mixer_ab():
            rmsnorm("nm_0")
            wab = dr["w_in_ab"].rearrange("(kc p) f -> p kc f", p=128)
            kdst = dr["k_s"] if pname == "S" else dr["k_p"]
            vdst = dr["v_s"] if pname == "S" else dr["v_p"]
            o_bufs = [view(U0 + 22528 + i * 2048, F32, [512]) for i in range(3)]
            if pname != "S":
                Qh = view(U0, BF16, [T])
                KTh = view(U0 + 2048, BF16, [2048])
                Vh = view(U0 + 6144, BF16, [16, 128])
                PT = [view(U0 + 10240 + i * 1024, BF16, [512]) for i in range(3)]
                TMP = [view(U0 + 13312 + i * 2048, F32, [512]) for i in range(2)]
                STG = [view(U0 + 17408 + i * 1024, F32, [256]) for i in range(2)]
                KA = view(U0 + 19456, BF16, [8, 128])
                nblk = T // 128
                blk0 = B0 // 128
                stgc = 0
                for h in range(NH):
                    wload(0, wab[:, :, 2048 + h * 128:2048 + (h + 1) * 128])
                    wload(1, wab[:, :, 3072 + h * 128:3072 + (h + 1) * 128])
                    wload(2, wab[:, :, 4096 + h * 128:4096 + (h + 1) * 128])
                    for ti, (t0, n) in enumerate(tiles):
                        bq = proj(0, 0, t0, n, ti)
                        P.op("act", lambda e, bq=bq, t0=t0, n=n: e.mul(out=Qh[:, t0:t0 + n], in_=bq[:, 0:n], mul=0.125),
                             r=["ps0"], w=["Qh"])
                        bk = proj(1, 1, t0, n, ti)
                        P.op("dve", lambda e, bk=bk, t0=t0, n=n: e.tensor_copy(out=KTh[:, B0 + t0:B0 + t0 + n], in_=bk[:, 0:n]),
                             r=["ps1"], w=["KTh"])
                    for bi in range(nblk if not st.get("no_tm") else 0):
                        k = 2 + bi % 2
                        bank = ps[k]
                        tb = bi * 128
                        ti = tb // 512
                        fns = []
                        for kc in range(16):
                            fns.append(lambda e, bank=bank, kc=kc, tb=tb: e.matmul(
                                bank[:, 0:128], lhsT=xn[:, kc, tb:tb + 128], rhs=WMs[1][:, kc, :], start=(kc == 0), stop=(kc == 15)))
                        P.group("pe", fns, r=["wm1", f"xn{ti}"], w=[f"ps{k}"])
                        fns = []
                        for kc in range(16):
                            fns.append(lambda e, bank=bank, kc=kc, tb=tb: e.matmul(
                                bank[:, 128:256], lhsT=xn[:, kc, tb:tb + 128], rhs=WMs[2][:, kc, :], start=(kc == 0), stop=(kc == 15)))
                        P.group("pe", fns, r=["wm2", f"xn{ti}"], w=[f"ps{k}"])
                        sg = STG[stgc % 2]
                        sname = f"kvstg{stgc % 2}"
                        stgc += 1
                        P.op("act", lambda e, bank=bank, sg=sg: e.copy(out=sg, in_=bank[:, 0:256]), r=[f"ps{k}"], w=[sname])
                        P.op("dve", lambda e, bank=bank, bi=bi: e.tensor_copy(out=Vh[:, blk0 + bi, :], in_=bank[:, 128:256]),
                             r=[f"ps{k}"], w=["Vh"])
                        r0 = ROW0 + B0 + tb
                        if not st.get("no_kvstore"):
                            P.dma("sp", kdst[r0:r0 + 128, h * 128:(h + 1) * 128], sg[:, 0:128], "k" + sname, r=[sname])
                            P.dma("sp", vdst[r0:r0 + 128, h * 128:(h + 1) * 128], sg[:, 128:256], "v" + sname, r=[sname])
                    if pname == "B":
                        P.dma("pool", Vh[:, 0:8, :], vdst[ROW0:ROW0 + 1024, h * 128:(h + 1) * 128].rearrange(
                            "(b p) d -> p b d", p=128), "Vh", w=["Vh"])
                        P.dma("pool", KA, kdst[ROW0:ROW0 + 1024, h * 128:(h + 1) * 128].rearrange(
                            "(b p) d -> p b d", p=128), "KA", w=["KA"])
                        for q4 in range(2):
                            bank = ps[2 + q4]
                            bkb = bank[:, 0:256].bitcast(BF16)
                            P.group("pe", [lambda e, bkb=bkb, q4=q4, j=j: e.transpose(
                                bkb[:, j * 128:(j + 1) * 128], KA[:, q4 * 4 + j, :], ident_b) for j in range(4)],
                                r=["KA", "identb"], w=[f"ps{2 + q4}"])
                            P.op("act", lambda e, bkb=bkb, q4=q4: e.copy(out=KTh[:, q4 * 512:(q4 + 1) * 512], in_=bkb),
                                 r=[f"ps{2 + q4}"], w=["KTh"])
                    slope = SLOPES[h]
                    for ti, (t0, n) in enumerate(tiles if st.get("att_level", 3) >= 2 else []):
                        qs = B0 + t0
                        nkt = (qs + n) // 128
                        for m in range(2):
                            zb, ub = ps[3 + m], ps[5 + m]
                            zname, uname = f"ps{3 + m}", f"ps{5 + m}"
                            blocks = list(range(nkt))
                            def s_mm(kt, m=m, t0=t0, n=n):
                                sb = kt % 3
                                P.op("pe", lambda e, sb=sb, kt=kt, m=m, t0=t0, n=n: e.matmul(
                                    ps[sb][:, 0:n], lhsT=KTh[m * 64:(m + 1) * 64, kt * 128:(kt + 1) * 128],
                                    rhs=Qh[m * 64:(m + 1) * 64, t0:t0 + n], start=True, stop=True),
                                    r=["KTh", "Qh"], w=[f"ps{sb}"])
                            def soft(kt, qs=qs, n=n):
                                sb = kt % 3
                                j = qs - kt * 128
                                if j >= 128:
                                    dt_, bias = dtile[0], -slope * j
                                else:
                                    dt_, bias = dtile[1 + (-j) // 128], 0.0
                                tmp = TMP[kt % 2]
                                pt = PT[kt % 3]
                                P.op("dve", lambda e, sb=sb, dt_=dt_, tmp=tmp, n=n, slope=slope: e.scalar_tensor_tensor(
                                    out=tmp[:, 0:n], in0=dt_[:, 0:n], scalar=-slope, in1=ps[sb][:, 0:n],
                                    op0=ALU.mult, op1=ALU.add), r=[f"ps{sb}", "dtiles"], w=[f"tmp{kt % 2}"])
                                P.op("act", lambda e, tmp=tmp, pt=pt, bias=bias, n=n: e.activation(
                                    out=pt[:, 0:n], in_=tmp[:, 0:n], func=AF.Exp, bias=float(bias), scale=1.0),
                                    r=[f"tmp{kt % 2}"], w=[f"pt{kt % 3}"])
                            def zu_mm(kt, idx, m=m, n=n, zb=zb, ub=ub, zname=zname, uname=uname):
                                pt = PT[kt % 3]
                                first, last = (idx == 0), (idx == nkt - 1)
                                P.op("pe", lambda e, pt=pt, n=n, zb=zb, first=first, last=last: e.matmul(
                                    zb[:, 0:n], lhsT=ones_b, rhs=pt[:, 0:n], start=first, stop=last),
                                    r=[f"pt{kt % 3}", "onesb"], w=[zname] if first else [])
                                P.op("pe", lambda e, pt=pt, n=n, ub=ub, kt=kt, first=first, last=last: e.matmul(
                                    ub[:, 0:n], lhsT=Vh[:, kt, :], rhs=pt[:, 0:n], start=first, stop=last),
                                    r=[f"pt{kt % 3}", "Vh"], w=[uname] if first else [])
                            s_mm(0)
                            if nkt > 1:
                                s_mm(1)
                            for idx, kt in enumerate(blocks):
                                soft(kt)
                                zu_mm(kt, idx)
                                if kt + 2 < nkt:
                                    s_mm(kt + 2)
                        for bn_ in ("ps3", "ps4", "ps5", "ps6"):
                            P.lastw[bn_] = (f"pe@{P.epoch}", P.cnt["pe"])
                        if st.get("att_level", 3) >= 3:
                            combine(ps[5][:, 0:n], ps[6][:, 0:n], ps[3][:, 0:n], ps[4][:, 0:n], AB[:, h, t0:t0 + n], n, None, o_bufs)
            else:
                sample_attention(wab, kdst, vdst, o_bufs)
            if not st.get("no_att"):
                out_proj(dr["w_out_ab"][1024:2048, :], 8, AB, "att")
            P.barrier()
            if not st.get("no_conv"):
                conv_branch(wab)
                out_proj(dr["w_out_ab"][0:1024, :], 8, AB, "att")


        def sample_attention(wab, kdst, vdst, o_bufs):
            S0 = X_OFF + 8192
            QT = view(S0, BF16, [8, 16])
            KTn = view(S0 + 256, BF16, [8, 16])
            Vn = view(S0 + 512, BF16, [4, 1024])
            QB = view(S0 + 8704, BF16, [8, 8])
            IDX = view(S0 + 8832, I32, [256])
            PTB = view(S0 + 9856, I32, [256])
            SB0 = view(S0 + 10880, F32, [64])
            SLV = view(S0 + 11136, F32, [64])
            SBN = view(S0 + 11392, F32, [64])
            KP = [view(S0 + 12288 + i * 2048, BF16, [1024]) for i in range(2)]
            VP = [view(S0 + 16384 + i * 2048, BF16, [1024]) for i in range(2)]
            KTp = view(S0 + 20480, BF16, [8, 128])
            TM1 = view(S0 + 22528, F32, [64])
            PTs = [view(S0 + 22784 + i * 128, BF16, [64]) for i in range(2)]
            STG = view(S0 + 23040, F32, [1024])
            P.dma("sp", PTB, dr["pt"][:, SI * 256:(SI + 1) * 256].partition_broadcast(128), "PTB", w=["PTB"])
            P.op("pool", lambda e: e.iota(IDX, pattern=[[0, 256]], base=0, channel_multiplier=1), w=["IDX"])
            IDXF = view(S0 + 27136, F32, [256])
            PTBF = view(S0 + 28160, F32, [256])
            P.op("dve", lambda e: e.tensor_copy(out=PTBF, in_=PTB), r=["PTB"], w=["PTBF"])
            P.op("dve", lambda e: e.tensor_copy(out=IDXF, in_=IDX), r=["IDX"], w=["IDXF"])
            P.op("dve", lambda e: e.scalar_tensor_tensor(out=IDXF, in0=PTBF, scalar=128.0, in1=IDXF, op0=ALU.mult, op1=ALU.add),
                 r=["PTBF", "IDXF"], w=["IDXF"])
            P.op("dve", lambda e: e.tensor_copy(out=IDX, in_=IDXF), r=["IDXF"], w=["IDX"])
            for h in range(NH):
                for m in range(2):
                    c0 = h * 8 + m * 4
                    P.op("dve", lambda e, c0=c0, h=h: e.tensor_scalar_mul(out=SB0[:, c0:c0 + 4], in0=dtile[0][:, 0:4],
                                                                          scalar1=-SLOPES[h]), r=["dtiles"], w=["SB0"])
                    P.op("dve", lambda e, c0=c0, h=h: e.memset(SLV[:, c0:c0 + 4], SLOPES[h]), w=["SLV"])
                    P.op("dve", lambda e, c0=c0, h=h: e.tensor_scalar_mul(out=SBN[:, c0:c0 + 4], in0=dtile[1][:, 0:4],
                                                                          scalar1=-SLOPES[h]), r=["dtiles"], w=["SBN"])
            for h in range(NH):
                wload(0, wab[:, :, 2048 + h * 128:2048 + (h + 1) * 128])
                wload(1, wab[:, :, 3072 + h * 128:3072 + (h + 1) * 128])
                bq = proj(0, 0, 0, 16, 0)
                P.op("act", lambda e, bq=bq, h=h: e.mul(out=QT[:, h, :], in_=bq[:, 0:16], mul=0.125), r=["ps0"], w=["QT"])
                bk = proj(1, 1, 0, 16, 0)
                P.op("dve", lambda e, bk=bk, h=h: e.tensor_copy(out=KTn[:, h, :], in_=bk[:, 0:16]), r=["ps1"], w=["KTn"])
            wkv = dr["w_in_ab"].rearrange("(kc p) f -> p kc f", p=128)
            for half in range(4):
                WB = view(R_OFF, BF16, [16, 512])
                col0 = 3072 + half * 512
                P.dma("pool", WB, wkv[:, :, col0:col0 + 512], "wm0", w=["wm0", "wm1", "wm2", "wm3"])
                for r_ in range(4):
                    bank = ps[2 + r_ % 2]
                    P.group("pe", [lambda e, bank=bank, kc=kc, r_=r_, WB=WB: e.matmul(
                        bank[0:4, :], lhsT=xn[:, kc, r_ * 4:(r_ + 1) * 4], rhs=WB[:, kc, :], start=(kc == 0), stop=(kc == 15))
                        for kc in range(16)], r=["wm0", "xn0"], w=[f"ps{2 + r_ % 2}"])
                    c0 = (half % 2) * 512
                    P.op("act", lambda e, bank=bank, c0=c0: e.copy(out=STG[0:4, c0:c0 + 512], in_=bank[0:4, :]),
                         r=[f"ps{2 + r_ % 2}"], w=["sstg"])
                    if half >= 2:
                        P.op("dve", lambda e, bank=bank, r_=r_, c0=c0: e.tensor_copy(out=Vn[0:4, r_, c0:c0 + 512], in_=bank[0:4, :]),
                             r=[f"ps{2 + r_ % 2}"], w=["Vn"])
                    dst = kdst if half < 2 else vdst
                    rr = ROW0 + r_ * 4
                    P.dma("sp", dst[rr:rr + 4, c0:c0 + 512], STG[0:4, c0:c0 + 512], "sstg", r=["sstg"])
            Zb, Ub = ps[3], ps[4]
            for r_ in range(4):
                P.op("dve", lambda e: e.memset(QB.rearrange("p a b -> p (a b)"), 0.0), w=["QB"])
                for m in range(2):
                    P.op("dve", lambda e, m=m, r_=r_: e.tensor_copy(out=QB[m * 64:(m + 1) * 64, :, m * 4:(m + 1) * 4],
                                                                    in_=QT[m * 64:(m + 1) * 64, :, r_ * 4:(r_ + 1) * 4]),
                         r=["QT"], w=["QB"])
                for pg in range(NPG + 1):
                    new = (pg == NPG)
                    sl = pg % 2
                    sbk = pg % 3
                    nk = 4 if new else 128
                    if not new:
                        col = r_ * 64 + pg
                        P.gather(KP[sl], dr["cache_k"], IDX[:, col:col + 1], f"KP{sl}", r=["IDX"], w=[f"KP{sl}"])
                        P.gather(VP[sl], dr["cache_v"], IDX[:, col:col + 1], f"VP{sl}", r=["IDX"], w=[f"VP{sl}"])
                        ktb = ps[7][:, :].bitcast(BF16)
                        P.group("pe", [lambda e, ktb=ktb, sl=sl, h=h: e.transpose(
                            ktb[:, h * 128:(h + 1) * 128], KP[sl][:, h * 128:(h + 1) * 128], ident_b) for h in range(NH)],
                            r=[f"KP{sl}", "identb"], w=["ps7"])
                        P.op("act", lambda e, ktb=ktb: e.copy(out=KTp.rearrange("p a b -> p (a b)"), in_=ktb), r=["ps7"], w=["KTp"])
                    fns = []
                    for h in range(NH):
                        if new:
                            lhs = KTn[:, h, r_ * 4:(r_ + 1) * 4]
                        else:
                            lhs = KTp[:, h, :]
                        fns.append(lambda e, lhs=lhs, h=h, sbk=sbk, nk=nk: e.matmul(
                            ps[sbk][0:nk, h * 8:(h + 1) * 8], lhsT=lhs, rhs=QB[:, h, :], start=True, stop=True))
                    P.group("pe", fns, r=["KTp", "KTn", "QB"], w=[f"ps{sbk}"])
                    if new:
                        P.op("dve", lambda e, sbk=sbk: e.tensor_tensor(out=TM1[0:4, :], in0=ps[sbk][0:4, 0:64], in1=SBN[0:4, :],
                                                                        op=ALU.add), r=[f"ps{sbk}", "SBN"], w=["TM1"])
                    else:
                        j = float(PAST - 128 * pg)
                        P.op("dve", lambda e, sbk=sbk: e.tensor_tensor(out=TM1, in0=ps[sbk][:, 0:64], in1=SB0, op=ALU.add),
                             r=[f"ps{sbk}", "SB0"], w=["TM1"])
                        P.op("dve", lambda e, j=j: e.scalar_tensor_tensor(out=TM1, in0=SLV, scalar=-j, in1=TM1,
                                                                          op0=ALU.mult, op1=ALU.add), r=["SLV", "TM1"], w=["TM1"])
                    pt = PTs[pg % 2]
                    P.op("act", lambda e, pt=pt, nk=nk: e.activation(out=pt[0:nk, :], in_=TM1[0:nk, :], func=AF.Exp),
                         r=["TM1"], w=[f"spt{pg % 2}"])
                    first = (pg == 0)
                    fns = [lambda e, pt=pt, nk=nk, first=first, new=new: e.matmul(
                        Zb[:, 0:64], lhsT=ones_b[0:nk, :], rhs=pt[0:nk, :], start=first, stop=new)]
                    for h in range(NH):
                        if new:
                            lv = Vn[0:4, r_, h * 128:(h + 1) * 128]
                        else:
                            lv = VP[sl][:, h * 128:(h + 1) * 128]
                        fns.append(lambda e, lv=lv, pt=pt, nk=nk, h=h, first=first, new=new: e.matmul(
                            Ub[:, h * 8:(h + 1) * 8], lhsT=lv, rhs=pt[0:nk, h * 8:(h + 1) * 8], start=(first and h == 0), stop=(new and h == NH - 1)))
                    P.group("pe", fns, r=[f"spt{pg % 2}", f"VP{sl}", "Vn", "onesb"], w=["ps3", "ps4"] if first else [])
                for bn_ in ("ps3", "ps4"):
                    P.lastw[bn_] = (f"pe@{P.epoch}", P.cnt["pe"])
                Zv = Zb[:, 0:64].rearrange("p (h m q) -> p h m q", h=8, m=2)
                Uv = Ub[:, 0:64].rearrange("p (h m q) -> p h m q", h=8, m=2)
                combine_s(Uv[:, :, 0, :], Uv[:, :, 1, :], Zv[:, :, 0, :], Zv[:, :, 1, :], AB[:, :, r_ * 4:(r_ + 1) * 4], o_bufs)

        def combine_s(U1, U2, Z1, Z2, outap, rb):
            o1, o2, o3 = rb
            n = 32
            def v(a):
                return a[:, 0:n].rearrange("p (a b) -> p a b", a=8)
            P.op("dve", lambda e: e.reciprocal(out=v(o1), in_=Z1), r=["ps3"], w=["o1"])
            P.op("dve", lambda e: e.tensor_tensor(out=v(o1), in0=v(o1), in1=U1, op=ALU.mult), r=["o1", "ps4"], w=["o1"])
            P.op("dve", lambda e: e.reciprocal(out=v(o2), in_=Z2), r=["ps3"], w=["o2"])
            P.op("dve", lambda e: e.tensor_tensor(out=v(o2), in0=v(o2), in1=U2, op=ALU.mult), r=["o2", "ps4"], w=["o2"])
            P.op("dve", lambda e: e.scalar_tensor_tensor(out=o1[:, 0:n], in0=o2[:, 0:n], scalar=neglam[:, 0:1],
                                                         in1=o1[:, 0:n], op0=ALU.mult, op1=ALU.add),
                 r=["o1", "o2", "neglam"], w=["o1"])
            P.op("act", lambda e: e.activation(out=o2[:, 0:n], in_=o1[:, 0:n], func=AF.Square), r=["o1"], w=["o2"])
            P.op("pe", lambda e: e.matmul(ps[5][:, 0:n], lhsT=ones_f, rhs=o2[:, 0:n], start=True, stop=True),
                 r=["o2", "onesf"], w=["ps5"])
            P.op("dve", lambda e: e.tensor_scalar(out=o3[:, 0:n], in0=ps[5][:, 0:n], scalar1=1.0 / 128, scalar2=RMS_EPS,
                                                  op0=ALU.mult, op1=ALU.add), r=["ps5"], w=["o3"])
            P.op("act", lambda e: e.sqrt(out=o3[:, 0:n], in_=o3[:, 0:n]), r=["o3"], w=["o3"])
            P.op("dve", lambda e: e.reciprocal(out=o3[:, 0:n], in_=o3[:, 0:n]), r=["o3"], w=["o3"])
            P.op("dve", lambda e: e.tensor_scalar(out=o1[:, 0:n], in0=o1[:, 0:n], scalar1=cvc("subg"),
                                                  scalar2=1.0 - LAMBDA_INIT, op0=ALU.mult, op1=ALU.mult),
                 r=["o1", "cvec"], w=["o1"])
            P.op("dve", lambda e: e.tensor_tensor(out=outap, in0=v(o1), in1=v(o3), op=ALU.mult), r=["o1", "o3"], w=["att"])


        RR, LL = (4, 4) if pname == "S" else (1, T)

        def conv_branch(wab):
            G = view(U0, F32, [RR, 30 + LL])
            ACC = view(U0 + 4224, F32, [RR, LL])
            SGM = [view(U0 + 8320 + i * 2048, F32, [512]) for i in range(2)]
            S1 = view(U0 + 12416, F32, [T])
            S2 = view(U0 + 16512, F32, [T])
            CST = view(U0 + 20608, F32, [1024])
            TQ = view(U0 + 24704, F32, [512])
            cw = CV["caw"]
            if pname == "S":
                SCA = view(X_OFF + 40960, F32, [4, 1024])
                for r_ in range(4):
                    P.dma("sp", SCA[0:30, r_, :], dr["sca"][SI * 4 + r_], "SCA", w=["SCA"])
            for c in range(8):
                sv, sgt = (c % 2) * 2, (c % 2) * 2 + 1
                wload(sv, wab[:, :, c * 128:(c + 1) * 128])
                wload(sgt, wab[:, :, 1024 + c * 128:1024 + (c + 1) * 128])
                if pname == "A":
                    P.op("act", lambda e: e.memzero(G[:, 0, 0:30]), w=["G"])
                elif pname == "B":
                    P.op("act", lambda e, c=c: e.copy(out=G[:, 0, 0:30], in_=ghist[:, c, :]), r=["ghist"], w=["G"])
                else:
                    P.group("pe", [lambda e, r_=r_, c=c: e.transpose(ps[6][:, r_ * 32:r_ * 32 + 30], SCA[0:30, r_, c * 128:(c + 1) * 128],
                                                                 ident_f[0:30, 0:30]) for r_ in range(4)], r=["SCA", "identf"], w=["ps6"])
                    P.op("act", lambda e: e.copy(out=G[:, :, 0:30], in_=ps[6][:, 0:128].rearrange("p (a b) -> p a b", a=4)[:, :, 0:30]),
                         r=["ps6"], w=["G"])
                for ti, (t0, n) in enumerate(tiles):
                    kb = (ti % 2) * 2
                    bv = proj(sv, kb, t0, n, ti)
                    bg = proj(sgt, kb + 1, t0, n, ti)
                    P.op("act", lambda e, bg=bg, ti=ti, n=n: e.activation(out=SGM[ti % 2][:, 0:n], in_=bg[:, 0:n], func=AF.Sigmoid),
                         r=[f"ps{kb + 1}"], w=[f"sgm{ti % 2}"])
                    if pname == "S":
                        P.op("dve", lambda e, bv=bv, ti=ti: e.tensor_tensor(
                            out=G[:, :, 30:34], in0=bv[:, 0:16].rearrange("p (a b) -> p a b", a=4),
                            in1=SGM[ti % 2][:, 0:16].rearrange("p (a b) -> p a b", a=4), op=ALU.mult),
                            r=[f"ps{kb}", f"sgm{ti % 2}"], w=["G"])
                    else:
                        P.op("dve", lambda e, bv=bv, ti=ti, t0=t0, n=n: e.tensor_tensor(
                            out=G[:, 0, 30 + t0:30 + t0 + n], in0=bv[:, 0:n], in1=SGM[ti % 2][:, 0:n], op=ALU.mult),
                            r=[f"ps{kb}", f"sgm{ti % 2}"], w=["G"])
                P.op("dve", lambda e, c=c: e.tensor_scalar(out=ACC, in0=G[:, :, 0:LL], scalar1=cvc("caw", c * 31),
                                                           scalar2=cvc("cab", c), op0=ALU.mult, op1=ALU.add),
                     r=["G", "cvec"], w=["ACC"])
                for j in range(1, 31):
                    P.op("dve", lambda e, c=c, j=j: e.scalar_tensor_tensor(out=ACC, in0=G[:, :, j:j + LL], scalar=cvc("caw", c * 31 + j),
                                                                           in1=ACC, op0=ALU.mult, op1=ALU.add),
                         r=["G", "ACC", "cvec"], w=["ACC"])
                accf = ACC.rearrange("p a b -> p (a b)")
                if pname == "A":
                    P.op("act", lambda e, c=c: e.copy(out=ghist[:, c, :], in_=G[:, 0, LL:LL + 30]), r=["G"], w=["ghist"])
                elif pname == "B":
                    P.op("pe", lambda e: e.transpose(ps[6][0:30, 0:128], G[:, 0, LL:LL + 30], ident_f), r=["G", "identf"], w=["ps6"])
                    P.op("act", lambda e, c=c: e.copy(out=CST[0:30, c * 128:(c + 1) * 128], in_=ps[6][0:30, 0:128]), r=["ps6"], w=["CST"])
                else:
                    for r_ in range(4):
                        P.op("pe", lambda e, r_=r_: e.transpose(ps[6][0:30, 0:128], G[:, r_, 4:34], ident_f), r=["G", "identf"], w=["ps6"])
                        P.op("act", lambda e: e.copy(out=CST[0:30, 0:128], in_=ps[6][0:30, 0:128]), r=["ps6"], w=["CST"])
                        P.dma("sp", dr["ca_s"][SI * 4 + r_][:, c * 128:(c + 1) * 128], CST[0:30, 0:128], "CST", r=["CST"])
                if c == 0:
                    P.op("dve", lambda e: e.tensor_copy(out=S1, in_=accf), r=["ACC"], w=["S1"])
                else:
                    P.op("dve", lambda e: e.tensor_add(out=S1, in0=S1, in1=accf), r=["ACC", "S1"], w=["S1"])
                for ti, (t0, n) in enumerate(tiles):
                    P.op("act", lambda e, t0=t0, n=n: e.activation(out=TQ[:, 0:n], in_=accf[:, t0:t0 + n], func=AF.Square),
                         r=["ACC"], w=["TQ"])
                    if c == 0:
                        P.op("dve", lambda e, t0=t0, n=n: e.tensor_copy(out=S2[:, t0:t0 + n], in_=TQ[:, 0:n]), r=["TQ"], w=["S2"])
                    else:
                        P.op("dve", lambda e, t0=t0, n=n: e.tensor_add(out=S2[:, t0:t0 + n], in0=S2[:, t0:t0 + n], in1=TQ[:, 0:n]),
                             r=["TQ", "S2"], w=["S2"])
                P.op("act", lambda e, c=c: e.copy(out=AB[:, c, :], in_=accf), r=["ACC"], w=["att"])
            if pname == "B":
                P.dma("sp", dr["ca_p"][SEQI * 30:(SEQI + 1) * 30, :], CST[0:30, :], "CST", r=["CST"])
            for ti, (t0, n) in enumerate(tiles):
                P.op("pe", lambda e, t0=t0, n=n: e.matmul(ps[4][:, 0:n], lhsT=ones_f, rhs=S1[:, t0:t0 + n], start=True, stop=True),
                     r=["S1", "onesf"], w=["ps4"])
                P.op("pe", lambda e, t0=t0, n=n: e.matmul(ps[5][:, 0:n], lhsT=ones_f, rhs=S2[:, t0:t0 + n], start=True, stop=True),
                     r=["S2", "onesf"], w=["ps5"])
                P.op("act", lambda e, t0=t0, n=n: e.mul(out=S1[:, t0:t0 + n], in_=ps[4][:, 0:n], mul=1.0 / 1024), r=["ps4"], w=["S1"])
                P.op("dve", lambda e, t0=t0, n=n: e.tensor_tensor(out=TQ[:, 0:n], in0=S1[:, t0:t0 + n], in1=S1[:, t0:t0 + n], op=ALU.mult),
                     r=["S1"], w=["TQ"])
                P.op("dve", lambda e, t0=t0, n=n: e.scalar_tensor_tensor(out=S2[:, t0:t0 + n], in0=ps[5][:, 0:n], scalar=1.0 / 1024,
                                                                        in1=TQ[:, 0:n], op0=ALU.mult, op1=ALU.subtract),
                     r=["ps5", "TQ"], w=["S2"])
                P.op("dve", lambda e, t0=t0, n=n: e.tensor_scalar_add(out=S2[:, t0:t0 + n], in0=S2[:, t0:t0 + n], scalar1=LN_EPS),
                     r=["S2"], w=["S2"])
                P.op("act", lambda e, t0=t0, n=n: e.sqrt(out=S2[:, t0:t0 + n], in_=S2[:, t0:t0 + n]), r=["S2"], w=["S2"])
                P.op("dve", lambda e, t0=t0, n=n: e.reciprocal(out=S2[:, t0:t0 + n], in_=S2[:, t0:t0 + n]), r=["S2"], w=["S2"])
            for c in range(8):
                for ti, (t0, n) in enumerate(tiles):
                    P.op("dve", lambda e, c=c, t0=t0, n=n: e.tensor_tensor(out=TQ[:, 0:n], in0=AB[:, c, t0:t0 + n], in1=S1[:, t0:t0 + n],
                                                                          op=ALU.subtract), r=["att", "S1"], w=["TQ"])
                    P.op("dve", lambda e, t0=t0, n=n: e.tensor_tensor(out=TQ[:, 0:n], in0=TQ[:, 0:n], in1=S2[:, t0:t0 + n], op=ALU.mult),
                         r=["TQ", "S2"], w=["TQ"])
                    P.op("act", lambda e, c=c, t0=t0, n=n: e.activation(out=AB[:, c, t0:t0 + n], in_=TQ[:, 0:n], func=AF.Silu,
                                                                        bias=cvc("lnb", c), scale=cvc("lng", c)),
                         r=["TQ", "cvec"], w=["att"])

        def mixer_c():
            rmsnorm("nm_1")
            wc = dr["w_in_c"].rearrange("(kc p) f -> p kc f", p=128)
            YC = view(R_OFF + 24576, BF16, [16, T])
            UU = view(R_OFF + 57344, F32, [RR, 2 + LL])
            CVT = view(R_OFF + 61448, F32, [512])
            CCS = view(R_OFF + 63496, F32, [128])
            if pname == "S":
                SCC = view(X_OFF + 8192, F32, [4, 2048])
                for r_ in range(4):
                    P.dma("sp", SCC[0:2, r_, :], dr["scc"][SI * 4 + r_], "SCC", w=["SCC"])
            for c in range(16):
                s0 = (c % 2) * 3
                for j in range(3):
                    wload(s0 + j, wc[:, :, j * 2048 + c * 128:j * 2048 + (c + 1) * 128])
                if pname == "A":
                    P.op("act", lambda e: e.memzero(UU[:, 0, 0:2]), w=["UU"])
                elif pname == "B":
                    P.op("act", lambda e, c=c: e.copy(out=UU[:, 0, 0:2], in_=uhist[:, c, :]), r=["uhist"], w=["UU"])
                else:
                    P.group("pe", [lambda e, r_=r_, c=c: e.transpose(ps[6][:, r_ * 2:r_ * 2 + 2], SCC[0:2, r_, c * 128:(c + 1) * 128],
                                                                 ident_f[0:2, 0:2]) for r_ in range(4)], r=["SCC", "identf"], w=["ps6"])
                    P.op("act", lambda e: e.copy(out=UU[:, :, 0:2], in_=ps[6][:, 0:8].rearrange("p (a b) -> p a b", a=4)),
                         r=["ps6"], w=["UU"])
                for ti, (t0, n) in enumerate(tiles):
                    kb = (ti % 2) * 3
                    b_b = proj(s0, kb, t0, n, ti)
                    b_c = proj(s0 + 1, kb + 1, t0, n, ti)
                    b_h = proj(s0 + 2, kb + 2, t0, n, ti)
                    if pname == "S":
                        uo = UU[:, :, 2:6]
                        def v3(a):
                            return a.rearrange("p (a b) -> p a b", a=4)
                        cvo = v3(CVT[:, 0:16])
                        P.op("act", lambda e, b_c=b_c: e.copy(out=uo, in_=v3(b_c[:, 0:16])), r=[f"ps{kb + 1}"], w=["UU"])
                        P.op("dve", lambda e, b_h=b_h: e.tensor_tensor(out=uo, in0=uo, in1=v3(b_h[:, 0:16]), op=ALU.mult),
                             r=[f"ps{kb + 2}", "UU"], w=["UU"])
                        taps = [UU[:, :, j:j + 4] for j in range(3)]
                        bbv = v3(b_b[:, 0:16])
                        ycv = v3(YC[:, c, 0:16])
                    else:
                        uo = UU[:, 0, 2 + t0:2 + t0 + n]
                        cvo = CVT[:, 0:n]
                        P.op("act", lambda e, b_c=b_c, uo=uo, n=n: e.copy(out=uo, in_=b_c[:, 0:n]), r=[f"ps{kb + 1}"], w=["UU"])
                        P.op("dve", lambda e, b_h=b_h, uo=uo, n=n: e.tensor_tensor(out=uo, in0=uo, in1=b_h[:, 0:n], op=ALU.mult),
                             r=[f"ps{kb + 2}", "UU"], w=["UU"])
                        taps = [UU[:, 0, t0 + j:t0 + j + n] for j in range(3)]
                        bbv = b_b[:, 0:n]
                        ycv = YC[:, c, t0:t0 + n]
                    P.op("dve", lambda e, c=c, cvo=cvo, taps=taps: e.tensor_scalar_mul(out=cvo, in0=taps[0], scalar1=cvc("ccw", c * 3)),
                         r=["UU", "cvec"], w=["CVT"])
                    for j in (1, 2):
                        P.op("dve", lambda e, c=c, j=j, cvo=cvo, taps=taps: e.scalar_tensor_tensor(
                            out=cvo, in0=taps[j], scalar=cvc("ccw", c * 3 + j), in1=cvo, op0=ALU.mult, op1=ALU.add),
                            r=["UU", "CVT", "cvec"], w=["CVT"])
                    P.op("dve", lambda e, cvo=cvo, bbv=bbv, ycv=ycv: e.tensor_tensor(out=ycv, in0=cvo, in1=bbv, op=ALU.mult),
                         r=["CVT", f"ps{kb}"], w=["YC"])
                if pname == "A":
                    P.op("act", lambda e, c=c: e.copy(out=uhist[:, c, :], in_=UU[:, 0, LL:LL + 2]), r=["UU"], w=["uhist"])
                elif pname == "B":
                    P.op("pe", lambda e: e.transpose(ps[7][0:2, 0:128], UU[:, 0, LL:LL + 2], ident_f), r=["UU", "identf"], w=["ps7"])
                    P.op("act", lambda e: e.copy(out=CCS[0:2, :], in_=ps[7][0:2, 0:128]), r=["ps7"], w=["CCS"])
                    P.dma("sp", dr["cc_p"][SEQI * 2:(SEQI + 1) * 2, c * 128:(c + 1) * 128], CCS[0:2, :], "CCS", r=["CCS"], w=["CCS"])
                else:
                    for r_ in range(4):
                        P.op("pe", lambda e, r_=r_: e.transpose(ps[7][0:2, 0:128], UU[:, r_, 4:6], ident_f), r=["UU", "identf"], w=["ps7"])
                        P.op("act", lambda e: e.copy(out=CCS[0:2, :], in_=ps[7][0:2, 0:128]), r=["ps7"], w=["CCS"])
                        P.dma("sp", dr["cc_s"][SI * 4 + r_][:, c * 128:(c + 1) * 128], CCS[0:2, :], "CCS", r=["CCS"], w=["CCS"])
            out_proj(dr["w_out_c"], 16, YC, "YC")

        load_x()
        P.barrier()
        nlayers = st.get("nlayers", 2)
        for layer in range(nlayers):
            if st.get("ffn", True):
                ffn(layer, 1)
                P.barrier()
            if st.get("mixer", False):
                if layer == 0:
                    mixer_ab()
                else:
                    mixer_c()
                P.barrier()
            if st.get("ffn", True) and st.get("ffn2", True):
                ffn(layer, 2)
                P.barrier()
        final_store()
        P.barrier(new_epoch=True)

    NSEQ, NSP = st["nseq"], st["nsp"]
    for q in range(NSEQ):
        for pname in ("A", "B"):
            if pname in st.get("passes", ("A", "B", "S")):
                one_pass(pname, SEQI=q)
    if "S" in st.get("passes", ("A", "B", "S")):
        for si in range(NSP):
            one_pass("S", SI=si)


def _cvec_host(inp):
    cv = np.zeros((128, NCV), np.float32)

    def fm(v):
        return np.ascontiguousarray(np.asarray(v, np.float32).reshape(-1, 128).T)

    for l in range(2):
        cv[:, CV[f"nf1_{l}"]:CV[f"nf1_{l}"] + 16] = fm(inp["norm_ffn1"][l])
        cv[:, CV[f"nm_{l}"]:CV[f"nm_{l}"] + 16] = fm(inp["norm_mix"][l])
        cv[:, CV[f"nf2_{l}"]:CV[f"nf2_{l}"] + 16] = fm(inp["norm_ffn2"][l])
    cv[:, CV["fin"]:CV["fin"] + 16] = fm(inp["final_norm"])
    caw = np.asarray(inp["conv_a_w"], np.float32)
    cv[:, CV["caw"]:CV["caw"] + 248] = caw.T.reshape(8, 128, 31).transpose(1, 0, 2).reshape(128, 248)
    cv[:, CV["cab"]:CV["cab"] + 8] = fm(inp["conv_a_b"])
    cv[:, CV["lng"]:CV["lng"] + 8] = fm(inp["conv_a_ln_g"])
    cv[:, CV["lnb"]:CV["lnb"] + 8] = fm(inp["conv_a_ln_b"])
    ccw = np.asarray(inp["conv_c_w"], np.float32)
    cv[:, CV["ccw"]:CV["ccw"] + 48] = ccw.T.reshape(16, 128, 3).transpose(1, 0, 2).reshape(128, 48)
    cv[:, CV["subg"]] = np.asarray(inp["diff_subln_g"], np.float32)
    cv[0:64, CV["dl"]:CV["dl"] + 4] = np.asarray(inp["diff_lambda"], np.float32).T
    return cv


def make_in_maps(inp, ncu):
    nseq = 8 // ncu
    nsp = 8 // ncu
    cv = _cvec_host(inp)
    shared = {}
    for nm in ("ffn1_w_gate", "ffn1_w_up", "ffn1_w_down", "ffn2_w_gate", "ffn2_w_up", "ffn2_w_down",
               "w_in_ab", "w_out_ab", "w_in_c", "w_out_c"):
        shared[nm] = np.ascontiguousarray(np.asarray(inp[nm], np.float32))
    shared["cache_k"] = np.asarray(inp["cache_k"], np.float32).reshape(NPHYS * 128, 1024)
    shared["cache_v"] = np.asarray(inp["cache_v"], np.float32).reshape(NPHYS * 128, 1024)
    shared["cvec"] = cv
    maps = []
    xs = np.asarray(inp["x_sample"], np.float32)
    xp = np.asarray(inp["x_prompt"], np.float32)
    for c in range(ncu):
        m = dict(shared)
        m["x_p"] = np.ascontiguousarray(xp[c * nseq:(c + 1) * nseq].reshape(nseq * SEQ, D))
        r0, r1 = c * nsp * 4, (c + 1) * nsp * 4
        m["x_s"] = np.ascontiguousarray(xs[r0:r1].reshape(nsp * 16, D))
        m["sca"] = np.ascontiguousarray(np.asarray(inp["state_conv_a"][r0:r1], np.float32))
        m["scc"] = np.ascontiguousarray(np.asarray(inp["state_conv_c"][r0:r1], np.float32))
        m["pt"] = np.ascontiguousarray(np.asarray(inp["page_table"][r0:r1], np.int32).reshape(1, nsp * 256))
        maps.append(m)
    return maps


NCU = 2


def kernel(**inp):
    ncu = NCU
    if "nc" not in _NC_CACHE:
        _NC_CACHE["nc"] = build_program(nseq=8 // ncu, nsp=8 // ncu)
    nc = _NC_CACHE["nc"]
    maps = make_in_maps(inp, ncu)
    res = run_bass_kernel_spmd(nc, maps, core_ids=list(range(ncu)))
    R = res.results

    def cat(name):
        return np.concatenate([np.asarray(R[c][name]) for c in range(ncu)], axis=0)

    y_p = cat("y_p").reshape(8, SEQ, D)
    y_s = cat("y_s").reshape(32, 4, D)
    k_p = cat("k_p").reshape(8, SEQ, NH, 128)
    v_p = cat("v_p").reshape(8, SEQ, NH, 128)
    ca_p = cat("ca_p").reshape(8, 30, 1024)
    cc_p = cat("cc_p").reshape(8, 2, 2048)
    k_s = cat("k_s").reshape(32, 4, NH, 128)
    v_s = cat("v_s").reshape(32, 4, NH, 128)
    ca_s = cat("ca_s").reshape(32, 30, 1024)
    cc_s = cat("cc_s").reshape(32, 2, 2048)
    return (y_p, y_s, k_p, v_p, ca_p, cc_p, k_s, v_s, ca_s, cc_s)
```
